# Optimizing a Trainium2 kernel written in Bass

```python
import jax, jax.numpy as jnp
from jax import lax
import numpy as np

D_MODEL = 1024
BATCH = 8
SEQ = 4096
DEPTH = 4

CHUNK = 64
Q_BLOCK = 2 * CHUNK
HEAD_DIM = 64
SB_WIDTH = D_MODEL // 2
SB_HEADS = SB_WIDTH // HEAD_DIM
CONV_WIDTH = D_MODEL // 4
CONV_K = 3
POOL_WINDOWS = (2, 4, 8, 16)
POOL_GROUPS = len(POOL_WINDOWS)
POOL_WIDTH = D_MODEL // 4
POOL_GDIM = POOL_WIDTH // POOL_GROUPS
MIX_WIDTH = SB_WIDTH + CONV_WIDTH + POOL_WIDTH
IN_WIDTH = 3 * SB_WIDTH + 3 * CONV_WIDTH + POOL_WIDTH
D_FF = 2816
N_MOD = 9
EPS = 1e-6

kernel_name = "hybrid_sb_conv_pool_macaron_adaln"


def rms_norm(x, gain):
    x32 = x.astype(jnp.float32)
    y = x32 * lax.rsqrt(jnp.mean(x32 * x32, axis=-1, keepdims=True) + EPS)
    return y.astype(x.dtype) * gain.astype(x.dtype)


def modulate(h, shift, scale):
    return h * (1 + scale[:, None, :]) + shift[:, None, :]


def swiglu(h, w_gate, w_up, w_down):
    return (jax.nn.silu(h @ w_gate) * (h @ w_up)) @ w_down


def stick_breaking_attention(q, k, v):
    seq = q.shape[2]
    scale = q.shape[-1] ** -0.5
    outs = []
    for i in range(seq // Q_BLOCK):
        q0 = i * Q_BLOCK
        kv_len = q0 + Q_BLOCK
        qb = q[:, :, q0:kv_len]
        kb = k[:, :, :kv_len]
        vb = v[:, :, :kv_len]
        z = jnp.einsum('bhqd,bhkd->bhqk', qb, kb).astype(jnp.float32) * scale
        t_pos = q0 + jnp.arange(Q_BLOCK)[:, None]
        s_pos = jnp.arange(kv_len)[None, :]
        past = s_pos < t_pos
        log_keep = jnp.where(past, -jax.nn.softplus(z), 0.0)
        suffix = lax.cumsum(log_keep, axis=3, reverse=True) - log_keep
        log_a = jax.nn.log_sigmoid(z) + suffix
        a = jnp.where(past, jnp.exp(log_a), 0.0)
        outs.append(jnp.einsum('bhqk,bhkd->bhqd', a.astype(vb.dtype), vb))
    return jnp.concatenate(outs, axis=2)


def short_conv_mixer(u, gate_b, gate_c, conv_w):
    cu = gate_c * u
    y = lax.conv_general_dilated(
        cu, conv_w[:, None, :].astype(cu.dtype), window_strides=(1,),
        padding=[(CONV_K - 1, 0)], dimension_numbers=('NWC', 'WIO', 'NWC'),
        feature_group_count=cu.shape[-1])
    return gate_b * y


def multiscale_pool_mixer(p, pool_w, pool_scale):
    bsz, seq, _ = p.shape
    pg = p.reshape(bsz, seq, POOL_GROUPS, POOL_GDIM)
    cs = jnp.cumsum(pg.astype(jnp.float32), axis=1)
    t1 = jnp.arange(1, seq + 1, dtype=jnp.float32)
    pooled = []
    for g, w in enumerate(POOL_WINDOWS):
        c_g = cs[:, :, g]
        lagged = jnp.pad(c_g, ((0, 0), (w, 0), (0, 0)))[:, :seq]
        count = jnp.minimum(t1, float(w))[None, :, None]
        pooled.append((c_g - lagged) / count)
    pooled = jnp.stack(pooled, axis=2).astype(p.dtype) - pg
    y = jnp.einsum('bsgc,gcd->bsgd', pooled, pool_w).reshape(bsz, seq, POOL_WIDTH)
    return y * pool_scale


def setup_inputs(seed: int = 0) -> dict:
    key = jax.random.key(seed)
    ks = jax.random.split(key, 24)
    f32 = jnp.float32
    nrm = lambda k, shape, s: jax.random.normal(k, shape, f32) * s
    gain = lambda k, shape: 1.0 + 0.02 * jax.random.normal(k, shape, f32)
    L = DEPTH
    return {
        "x": jax.random.normal(ks[0], (BATCH, SEQ, D_MODEL), f32),
        "c": jax.random.normal(ks[1], (BATCH, D_MODEL), f32),
        "w_ada": nrm(ks[2], (L, D_MODEL, N_MOD * D_MODEL), 0.1 * D_MODEL ** -0.5),
        "b_ada": nrm(ks[3], (L, N_MOD * D_MODEL), 0.01),
        "ffn1_norm": gain(ks[4], (L, D_MODEL)),
        "ffn1_gate": nrm(ks[5], (L, D_MODEL, D_FF), D_MODEL ** -0.5),
        "ffn1_up": nrm(ks[6], (L, D_MODEL, D_FF), D_MODEL ** -0.5),
        "ffn1_down": nrm(ks[7], (L, D_FF, D_MODEL), D_FF ** -0.5),
        "mix_norm": gain(ks[8], (L, D_MODEL)),
        "w_in": nrm(ks[9], (L, D_MODEL, IN_WIDTH), D_MODEL ** -0.5),
        "q_norm": gain(ks[10], (L, HEAD_DIM)),
        "k_norm": gain(ks[11], (L, HEAD_DIM)),
        "conv_w": nrm(ks[12], (L, CONV_K, CONV_WIDTH), CONV_K ** -0.5),
        "pool_w": nrm(ks[13], (L, POOL_GROUPS, POOL_GDIM, POOL_GDIM), POOL_GDIM ** -0.5),
        "pool_scale": gain(ks[14], (L, POOL_WIDTH)),
        "w_out": nrm(ks[15], (L, MIX_WIDTH, D_MODEL), MIX_WIDTH ** -0.5),
        "ffn2_norm": gain(ks[16], (L, D_MODEL)),
        "ffn2_gate": nrm(ks[17], (L, D_MODEL, D_FF), D_MODEL ** -0.5),
        "ffn2_up": nrm(ks[18], (L, D_MODEL, D_FF), D_MODEL ** -0.5),
        "ffn2_down": nrm(ks[19], (L, D_FF, D_MODEL), D_FF ** -0.5),
    }


def reference(x, c, w_ada, b_ada, ffn1_norm, ffn1_gate, ffn1_up, ffn1_down,
              mix_norm, w_in, q_norm, k_norm, conv_w, pool_w, pool_scale, w_out,
              ffn2_norm, ffn2_gate, ffn2_up, ffn2_down):
    bsz, seq, _ = x.shape
    splits = [SB_WIDTH, 2 * SB_WIDTH, 3 * SB_WIDTH,
              3 * SB_WIDTH + CONV_WIDTH, 3 * SB_WIDTH + 2 * CONV_WIDTH,
              3 * SB_WIDTH + 3 * CONV_WIDTH]
    cond = jax.nn.silu(c)

    def to_heads(t):
        return t.reshape(bsz, seq, SB_HEADS, HEAD_DIM).transpose(0, 2, 1, 3)

    for l in range(DEPTH):
        mod = cond @ w_ada[l] + b_ada[l]
        sh1, sc1, g1, sh2, sc2, g2, sh3, sc3, g3 = jnp.split(mod, N_MOD, axis=-1)

        h = modulate(rms_norm(x, ffn1_norm[l]), sh1, sc1)
        x = x + 0.5 * (1 + g1)[:, None, :] * swiglu(h, ffn1_gate[l], ffn1_up[l], ffn1_down[l])

        h = modulate(rms_norm(x, mix_norm[l]), sh2, sc2)
        proj = h @ w_in[l]
        q, k, v, cb, cc, cu, p = jnp.split(proj, splits, axis=-1)
        q = rms_norm(to_heads(q), q_norm[l])
        k = rms_norm(to_heads(k), k_norm[l])
        y_sb = stick_breaking_attention(q, k, to_heads(v))
        y_sb = y_sb.transpose(0, 2, 1, 3).reshape(bsz, seq, SB_WIDTH)
        y_conv = short_conv_mixer(cu, cb, cc, conv_w[l])
        y_pool = multiscale_pool_mixer(p, pool_w[l], pool_scale[l])
        mixed = jnp.concatenate([y_sb, y_conv, y_pool], axis=-1) @ w_out[l]
        x = x + (1 + g2)[:, None, :] * mixed

        h = modulate(rms_norm(x, ffn2_norm[l]), sh3, sc3)
        x = x + 0.5 * (1 + g3)[:, None, :] * swiglu(h, ffn2_gate[l], ffn2_up[l], ffn2_down[l])
    return x
```

```python
import numpy as np
import concourse.bass as bass
import concourse.mybir as mybir
from concourse.bass_utils import run_bass_kernel_spmd
from contextlib import ExitStack

F32 = mybir.dt.float32
BF16 = mybir.dt.bfloat16
AF = mybir.ActivationFunctionType
ALU = mybir.AluOpType

L = 4
D = 1024
T = 4096
FF = 2816
NT = 8
TT = 512
KC = 8
FC = 22
EPS = 1e-6
ENGS = ("pe", "act", "dve", "pool", "sp")
WQ2 = "pool"

C_NEGTRI, C_NEGONE, C_BLK, C_ONES, C_MASK = 0, 128, 256, 384, 512
C_BF_END = 512 + 4 * 512
C_INVCNT = C_BF_END
C_INVW = C_INVCNT + 32
NCST = C_INVW + 2


class Rec:
    def __init__(self, nc, es):
        self.nc = nc
        self.es = es
        self.q = {e: [] for e in ENGS}
        self.sem = {}
        self.cnt = {}
        for e in ("pe", "act", "dve", "pool"):
            self.sem[e] = es.enter_context(nc.semaphore("s_" + e))
            self.cnt[e] = 0
        self.seen = {e: {} for e in ENGS}
        self.w = {}
        self.r = {}
        self.ninstr = 0

    def chan(self, name):
        self.sem[name] = self.es.enter_context(self.nc.semaphore("d_" + name))
        self.cnt[name] = 0
        return name

    def _deps(self, reads, writes):
        deps = []
        for b in reads:
            if b in self.w:
                deps.append(self.w[b])
        for b in writes:
            if b in self.w:
                deps.append(self.w[b])
            deps.extend(self.r.get(b, ()))
        return deps

    def _commit(self, tok, reads, writes):
        if tok is None:
            return
        for b in reads:
            self.r.setdefault(b, []).append(tok)
        for b in writes:
            self.w[b] = tok
            self.r[b] = []

    def _wait(self, eng, deps):
        best = {}
        for key, val in deps:
            if eng == "pe" and key == "pe":
                continue
            if val > best.get(key, 0):
                best[key] = val
        for key, val in best.items():
            if self.seen[eng].get(key, 0) >= val:
                continue
            self.seen[eng][key] = val
            sem = self.sem[key]
            self.q[eng].append(lambda e, sem=sem, val=val: e.wait_ge(sem, val))

    def op(self, eng, fn, reads=(), writes=(), sig=True):
        self._wait(eng, self._deps(reads, writes))
        self.ninstr += 1
        if sig:
            self.cnt[eng] += 1
            sem = self.sem[eng]
            self.q[eng].append(lambda e, fn=fn, sem=sem: fn(e).then_inc(sem, 1))
            tok = (eng, self.cnt[eng])
        else:
            self.q[eng].append(fn)
            tok = None
        self._commit(tok, reads, writes)
        return tok

    def mm(self, out, pairs, reads=(), writes=(), start=True, stop=True, **kw):
        self._wait("pe", self._deps(reads, writes))
        n = len(pairs)
        for i, (lhsT, rhs) in enumerate(pairs):
            st = start and i == 0
            sp = stop and i == n - 1
            self.ninstr += 1
            if i == n - 1:
                self.cnt["pe"] += 1
                sem = self.sem["pe"]
                self.q["pe"].append(lambda e, out=out, lhsT=lhsT, rhs=rhs, st=st, sp=sp, sem=sem, kw=kw:
                                    e.matmul(out, lhsT=lhsT, rhs=rhs, start=st, stop=sp, **kw).then_inc(sem, 1))
            else:
                self.q["pe"].append(lambda e, out=out, lhsT=lhsT, rhs=rhs, st=st, sp=sp, kw=kw:
                                    e.matmul(out, lhsT=lhsT, rhs=rhs, start=st, stop=sp, **kw))
        tok = ("pe", self.cnt["pe"])
        self._commit(tok, reads, writes)
        return tok

    def dma(self, qeng, ch, out, in_, reads=(), writes=()):
        self._wait(qeng, self._deps(reads, writes))
        self.ninstr += 1
        self.cnt[ch] += 16
        sem = self.sem[ch]
        self.q[qeng].append(lambda e, out=out, in_=in_, sem=sem: e.dma_start(out=out, in_=in_).then_inc(sem, 16))
        tok = (ch, self.cnt[ch])
        self._commit(tok, reads, writes)
        return tok

    def group(self, names):
        last = max((self.w[b] for b in names), key=lambda t: t[1])
        for b in names:
            assert self.w[b][0] == last[0]
            self.w[b] = last

    def barrier(self):
        allk = [(k, v) for k, v in self.cnt.items() if v > 0]
        for e in ENGS:
            self._wait(e, [(k, v) for k, v in allk])
        self.w.clear()
        self.r.clear()


def f_act(out, in_, func, **kw):
    return lambda e: e.activation(out=out, in_=in_, func=func, **kw)


def f_tt(out, in0, in1, op):
    return lambda e: e.tensor_tensor(out=out, in0=in0, in1=in1, op=op)


def f_ts(out, in0, s1, s2, op0, op1):
    return lambda e: e.tensor_scalar(out=out, in0=in0, scalar1=s1, scalar2=s2, op0=op0, op1=op1)


def f_stt(out, in0, scalar, in1, op0, op1):
    return lambda e: e.scalar_tensor_tensor(out=out, in0=in0, scalar=scalar, in1=in1, op0=op0, op1=op1)


def f_copy(out, in_):
    return lambda e: e.tensor_copy(out=out, in_=in_)


def f_recip(out, in_):
    return lambda e: e.reciprocal(out=out, in_=in_)


def f_memset(ap, v):
    return lambda e: e.memset(ap, v)


def actb_xt2(view, o_actb):
    return view(o_actb, [KC, 512], F32)


def build(n_layers=L, stop_after=None):
    nc = bass.Bass("TRN2", target_bir_lowering=False)

    def din(name, shape):
        return nc.dram_tensor(name, shape, F32, kind="ExternalInput").ap()

    xT = din("xT", [KC, 128, T])
    ccol = din("ccol", [128, KC])
    cst = din("cst", [128, NCST])
    w_ada = din("w_ada", [L, D, 9 * D])
    b_adaT = din("b_adaT", [L, 128, 72])
    norms = din("norms", [L, 128, 24])
    fgate = [din("ffn1_gate", [L, D, FF]), din("ffn2_gate", [L, D, FF])]
    fup = [din("ffn1_up", [L, D, FF]), din("ffn2_up", [L, D, FF])]
    fdown = [din("ffn1_down", [L, FF, D]), din("ffn2_down", [L, FF, D])]
    w_in = din("w_in", [L, D, 2560])
    w_out = din("w_out", [L, D, D])
    qkg = din("qkg", [L, 128, 2])
    convw = din("convw", [L, 128, 6])
    pwblk = din("pwblk", [L, 128, 256])
    pscale = din("pscale", [L, 128, 2])
    yT = nc.dram_tensor("yT", [KC, 128, T], F32, kind="ExternalOutput").ap()
    xs_d = nc.dram_tensor("xscr", [KC, 128, T], F32, kind="Internal").ap()

    with ExitStack() as es:
        ARENA_B = 212480
        arena = es.enter_context(nc.sbuf_tensor("arena", [128, ARENA_B // 4], F32))
        psum = es.enter_context(nc.psum_tensor("ps", [128, 8, 512], F32))
        R = Rec(nc, es)
        for ch in ["wg%d" % j for j in range(6)] + ["wu%d" % j for j in range(6)] + ["wd%d" % j for j in range(6)] + ["wa0", "wa1", "xl0", "xl1", "xl2", "xs0", "xs1", "xst0", "xst1", "misc", "miscp"]:
            R.chan(ch)

        def view(off, shape, dtype):
            n = int(np.prod(shape))
            esz = 4 if dtype == F32 else 2
            assert off % 4 == 0 and (n * esz) % 4 == 0
            a = arena[:, off // 4:(off + n * esz) // 4]
            if dtype != F32:
                a = a.bitcast(dtype)
            if len(shape) == 2:
                a = a.rearrange("p (a b) -> p a b", a=shape[0])
            return a

        KB = 1024
        BIG = 0
        Wg = view(BIG, [KC, FF], BF16)
        Wu = view(BIG + 44 * KB, [KC, FF], BF16)
        Wd = view(BIG + 88 * KB, [FC, D], BF16)
        wa = [view(BIG + 88 * KB + s * 16 * KB, [KC, 1024], BF16) for s in range(2)]
        qb = view(BIG, [4, T], BF16)
        kb_ = view(BIG + 32 * KB, [4, T], BF16)
        vb = view(BIG + 64 * KB, [32, 512], BF16)
        Wqkv = view(BIG + 96 * KB, [KC, 1536], BF16)
        Wcp = view(BIG + 32 * KB, [KC, 1024], BF16)
        Wout = view(BIG + 48 * KB, [KC, 1024], BF16)
        Pw = view(BIG + 64 * KB, [2, 128], BF16)
        mo = BIG + 66 * KB
        cbs = view(mo, [2, 512], F32); mo += 4 * KB
        ccs = view(mo, [2, 512], F32); mo += 4 * KB
        ccu = view(mo, [2, 514], F32); mo += 2 * 514 * 4
        acc = view(mo, [512], F32); mo += 2 * KB
        pbuf = view(mo, [2, 528], F32); mo += 2 * 528 * 4
        s2 = view(mo, [528], F32); mo += 528 * 4
        s4 = view(mo, [528], F32); mo += 528 * 4
        s8 = view(mo, [528], F32); mo += 528 * 4
        s16 = view(mo, [528], F32); mo += 528 * 4
        pooled2 = view(mo, [2, 512], BF16); mo += 2 * KB
        s2b = view(mo, [528], F32); mo += 528 * 4
        s4b = view(mo, [528], F32); mo += 528 * 4
        ymix = view(mo, [4, 512], BF16); mo += 4 * KB
        assert mo <= 104 * KB, mo
        o = 132 * KB
        xt = view(o, [KC, 512], F32); o += 16 * KB
        xsb = view(o, [2, 512], F32); o += 4 * KB
        h = view(o, [KC, 512], BF16); o += 8 * KB
        o_actb = o
        actb = view(o, [FC, 512], BF16); o += 22 * KB
        sqk = view(o, [2, 512], BF16); o += 2 * KB
        tmp = view(o, [2, 512], F32); o += 4 * KB
        rt = view(o, [512], F32); o += 2 * KB
        rstd = view(o, [512], F32); o += 2 * KB
        sg = view(o, [2, 512], BF16); o += 2 * KB
        sq8 = [actb[:, 16 + j, :] for j in range(6)] + [sqk[:, 0, :], sqk[:, 1, :]]
        xts = [xt, actb_xt2(view, o_actb), view(BIG + 104 * KB, [KC, 512], F32)]
        hs = [h, view(BIG + 120 * KB, [KC, 512], BF16)]
        cbf = view(o, [C_BF_END], BF16); o += C_BF_END * 2
        cf32 = view(o, [34], F32); o += 34 * 4
        modv = view(o, [72], F32); o += 72 * 4
        bada = view(o, [72], F32); o += 72 * 4
        nrm = view(o, [24], F32); o += 24 * 4
        AG = view(o, [48], F32); o += 48 * 4
        qkgs = view(o, [4], F32); o += 16
        cws = view(o, [6], F32); o += 24
        pscs = view(o, [2], F32); o += 8
        ccs_ = view(o, [KC], F32); o += 32
        cond = view(o, [KC], BF16); o += 16
        assert o <= ARENA_B, o

        negtri = cbf[:, C_NEGTRI:C_NEGTRI + 128]
        negone = cbf[:, C_NEGONE:C_NEGONE + 128]
        blk = cbf[:, C_BLK:C_BLK + 128]
        ones = cbf[:, C_ONES:C_ONES + 128]
        mask = cbf[:, C_MASK:C_MASK + 2048].rearrange("p (a b) -> p a b", a=4)
        invcnt = cf32[:, 0:32].rearrange("p (a b) -> p a b", a=2)
        invw = cf32[:, 32:34]
        e3 = [xt[:, 2 * i:2 * i + 2, :] for i in range(3)]
        sp3 = [actb[:, 2 * i:2 * i + 2, :] for i in range(3)]
        a3 = [actb[:, 6 + 2 * i:8 + 2 * i, :] for i in range(3)]
        spsum = [actb[:, 12 + 2 * i:14 + 2 * i, :] for i in range(3)]

        def ps(b):
            return psum[:, b, :]

        R.dma("pool", "miscp", out=cbf[:, 0:1280], in_=cst[:, 0:1280], writes=["cbf0"])
        R.dma("pool", "miscp", out=cbf[:, 1280:C_BF_END], in_=cst[:, 1280:C_BF_END], writes=["cbf"])
        R.dma("sp", "misc", out=cf32, in_=cst[:, C_BF_END:NCST], writes=["cf32"])
        R.dma("sp", "misc", out=ccs_, in_=ccol, writes=["ccol"])
        R.group(["cbf0", "cbf"])
        R.group(["cf32", "ccol"])
        R.op("act", f_act(cond, ccs_, AF.Silu), reads=["ccol"], writes=["cond"])

        state = {"src": xT, "src_is_scr": False}

        def xname(i, dk):
            return ("X", i, dk)

        def load_x(i, b=0):
            src = state["src"]
            rd = [xname(i, dk) for dk in range(KC)] if state["src_is_scr"] else []
            R.dma("sp", "xl%d" % b, out=xts[b], in_=src.rearrange("k p t -> p k t")[:, :, i * TT:(i + 1) * TT],
                  reads=rd, writes=[("xt", b)])

        def norm(Aoff, Boff, b=0, bh=None):
            bx = b
            b = bx if bh is None else bh
            xb, hb = xts[bx], hs[b]
            for k in range(KC):
                sb = k % 2
                R.op("act", f_act(sqk[:, sb, :], xb[:, k, :], AF.Square), reads=[("xt", bx)], writes=[("sqk", sb)])
                R.mm(ps(6), [(ones, sqk[:, sb, :])], reads=[("sqk", sb), "cbf"], writes=[("ps", 6)],
                     start=(k == 0), stop=(k == KC - 1))
            R.op("act", f_act(rt, ps(6), AF.Ln, scale=1.0 / D, bias=EPS), reads=[("ps", 6)], writes=["rt"])
            R.op("act", f_act(rstd, rt, AF.Exp, scale=-0.5), reads=["rt"], writes=["rstd"])
            for k in range(KC):
                tb = k % 2
                R.op("dve", f_stt(tmp[:, tb, :], xb[:, k, :], AG[:, Aoff + k:Aoff + k + 1], rstd, ALU.mult, ALU.mult),
                     reads=[("xt", bx), "rstd", "AG"], writes=[("tmp", tb)])
                R.op("act", f_act(hb[:, k, :], tmp[:, tb, :], AF.Identity, bias=modv[:, Boff + k:Boff + k + 1], scale=1.0),
                     reads=[("tmp", tb), "modv"], writes=[("h", b, k)])

        def norm_sq(k, bx):
            R.op("pool", f_tt(sq8[k], xts[bx][:, k, :], xts[bx][:, k, :], ALU.mult), reads=[("xt", bx)], writes=[("sq8", k) if k < 6 else ("sqk", k - 6)])

        def norm_acc(k):
            R.mm(ps(6), [(ones, sq8[k])], reads=[("sq8", k) if k < 6 else ("sqk", k - 6), "cbf"], writes=[("ps", 6)],
                 start=(k == 0), stop=(k == KC - 1))

        def norm_rstd():
            R.op("act", f_act(rt, ps(6), AF.Ln, scale=1.0 / D, bias=EPS), reads=[("ps", 6)], writes=["rt"])
            R.op("act", f_act(rstd, rt, AF.Exp, scale=-0.5), reads=["rt"], writes=["rstd"])

        def norm_out(k, Aoff, Boff, bx, bh):
            tb = k % 2
            R.op("dve", f_stt(tmp[:, tb, :], xts[bx][:, k, :], AG[:, Aoff + k:Aoff + k + 1], rstd, ALU.mult, ALU.mult),
                 reads=[("xt", bx), "rstd", "AG"], writes=[("tmp", tb)])
            R.op("pool", f_ts(hs[bh][:, k, :], tmp[:, tb, :], 1.0, modv[:, Boff + k:Boff + k + 1], ALU.mult, ALU.add),
                 reads=[("tmp", tb), "modv"], writes=[("h", bh, k)])

        def H_ALLb(b):
            return [("h", b, k) for k in range(KC)]

        H_ALL = H_ALLb(0)

        def xupdate(i, dk, ob, Goff, dst, dst_is_scr):
            src = state["src"]
            xbi = dk % 2
            rd = [xname(i, dk)] if state["src_is_scr"] else []
            R.dma("sp", "xs%d" % xbi, out=xsb[:, xbi, :], in_=src[dk, :, i * TT:(i + 1) * TT], reads=rd, writes=[("xsb", xbi)])
            R.op("dve", f_stt(xsb[:, xbi, :], ps(ob), AG[:, Goff + dk:Goff + dk + 1], xsb[:, xbi, :], ALU.mult, ALU.add),
                 reads=[("ps", ob), ("xsb", xbi), "AG"], writes=[("xsb", xbi)])
            wr = [xname(i, dk)] if dst_is_scr else [("Y", i, dk)]
            R.dma("sp", "xst%d" % xbi, out=dst[dk, :, i * TT:(i + 1) * TT], in_=xsb[:, xbi, :], reads=[("xsb", xbi)], writes=wr)

        def ada_phase(l):
            R.dma("sp", "misc", out=bada, in_=b_adaT[l], writes=["bada"])
            R.dma("sp", "misc", out=nrm, in_=norms[l], writes=["nrm"])
            R.dma("sp", "misc", out=qkgs[:, 0:2], in_=qkg[l], writes=["qkgs"])
            R.dma("sp", "misc", out=cws, in_=convw[l], writes=["cws"])
            R.dma("sp", "misc", out=pscs, in_=pscale[l], writes=["pscs"])
            R.group(["bada", "nrm", "qkgs", "cws", "pscs"])
            wa_d = w_ada[l].rearrange("(k p) n -> p k n", p=128)
            for s in range(9):
                slot = s % 2
                R.dma("pool" if s % 2 == 0 else WQ2, "wa%d" % slot, out=wa[slot], in_=wa_d[:, :, s * 1024:(s + 1) * 1024], writes=[("wa", slot)])
                for j in range(8):
                    n = s * 8 + j
                    R.mm(psum[:, 7, n:n + 1], [(wa[slot][:, k, j * 128:(j + 1) * 128], cond[:, k:k + 1]) for k in range(KC)],
                         reads=[("wa", slot), "cond"], writes=[("ps", 7)])
            R.op("dve", f_tt(modv, psum[:, 7, 0:72], bada, ALU.add), reads=[("ps", 7), "bada"], writes=["modv"])
            for t in range(3):
                sc = modv[:, (3 * t + 1) * 8:(3 * t + 2) * 8]
                g = modv[:, (3 * t + 2) * 8:(3 * t + 3) * 8]
                R.op("dve", f_stt(AG[:, t * 16:t * 16 + 8], sc, 1.0, nrm[:, t * 8:(t + 1) * 8], ALU.add, ALU.mult),
                     reads=["modv", "nrm"], writes=["AG"])
                R.op("dve", f_ts(AG[:, t * 16 + 8:t * 16 + 16], g, 1.0, (1.0 if t == 1 else 0.5), ALU.add, ALU.mult),
                     reads=["modv"], writes=["AG"])
            R.op("dve", f_ts(qkgs[:, 2:3], qkgs[:, 0:1], 0.125, None, ALU.mult, ALU.bypass), reads=["qkgs"], writes=["qkgs"])

        def ffn_phase(l, which, dst, dst_is_scr):
            wi = which - 1
            t3 = 0 if which == 1 else 2
            Aoff, Boff, Goff = t3 * 16, (3 * t3) * 8, t3 * 16 + 8
            g_d = fgate[wi][l].rearrange("(k p) f -> p k f", p=128)
            u_d = fup[wi][l].rearrange("(k p) f -> p k f", p=128)
            d_d = fdown[wi][l].rearrange("(c p) d -> p c d", p=128)
            load_x(0)
            for s in range(6):
                c0, c1 = s * 512, min(FF, s * 512 + 512)
                R.dma("pool", "wg%d" % s, out=Wg[:, :, c0:c1], in_=g_d[:, :, c0:c1], writes=[("Wg", s)])
                R.dma(WQ2, "wu%d" % s, out=Wu[:, :, c0:c1], in_=u_d[:, :, c0:c1], writes=[("Wu", s)])
            for s in range(6):
                f0, f1 = s * 4, min(FC, s * 4 + 4)
                R.dma("pool", "wd%d" % s, out=Wd[:, f0:f1, :], in_=d_d[:, f0:f1, :], writes=[("Wd", s)])

            def gateup(i):
                for fc in range(FC):
                    gbk = fc % 2
                    sl = slice(fc * 128, (fc + 1) * 128)
                    R.mm(ps(gbk), [(Wg[:, k, sl], h[:, k, :]) for k in range(KC)],
                         reads=[("Wg", fc // 4)] + H_ALL, writes=[("ps", gbk)])
                    R.mm(ps(2 + gbk), [(Wu[:, k, sl], h[:, k, :]) for k in range(KC)],
                         reads=[("Wu", fc // 4)] + H_ALL, writes=[("ps", 2 + gbk)])
                    R.op("act", f_act(sg[:, gbk, :], ps(gbk), AF.Silu), reads=[("ps", gbk)], writes=[("sg", gbk)])
                    R.op("dve", f_tt(actb[:, fc, :], ps(2 + gbk), sg[:, gbk, :], ALU.mult),
                         reads=[("ps", 2 + gbk), ("sg", gbk)], writes=[("act", fc)])

            def down(i):
                for dk in range(KC):
                    ob = 4 + dk % 2
                    R.mm(ps(ob), [(Wd[:, fc, dk * 128:(dk + 1) * 128], actb[:, fc, :]) for fc in range(FC)],
                         reads=[("Wd", s) for s in range(6)] + [("act", fc) for fc in range(FC)], writes=[("ps", ob)])
                    xupdate(i, dk, ob, Goff, dst, dst_is_scr)

            norm(Aoff, Boff)
            for i in range(NT):
                gateup(i)
                if i + 1 < NT:
                    load_x(i + 1)
                    norm(Aoff, Boff)
                down(i)

        def proj_phase(l):
            win_d = w_in[l].rearrange("(k p) n -> p k n", p=128)
            for s in range(3):
                R.dma("pool", "wg%d" % s, out=Wqkv[:, :, s * 512:(s + 1) * 512], in_=win_d[:, :, s * 512:(s + 1) * 512],
                      writes=[("Wqkv", s)])
            rq = xsb
            load_x(0, 0)
            load_x(1, 1)
            norm(16, 24, 0)
            for i in range(NT):
                b = i % 2
                hb = hs[b]
                HB = H_ALLb(b)
                if i + 2 < NT:
                    load_x(i + 2, b)

                def head(c):
                    pb = c % 4
                    sb = c % 2
                    R.mm(ps(pb), [(Wqkv[:, k, c * 128:(c + 1) * 128], hb[:, k, :]) for k in range(KC)],
                         reads=[("Wqkv", c // 4)] + HB, writes=[("ps", pb)])
                    R.op("act", f_act(sg[:, sb, :], ps(pb), AF.Square), reads=[("ps", pb)], writes=[("sg", sb)])

                def blkmm(c):
                    sb = c % 2
                    R.mm(ps(4 + sb), [(blk, sg[:, sb, :])], reads=[("sg", sb), "cbf"], writes=[("ps", 4 + sb)])

                def epi(c):
                    pb = c % 4
                    sb = c % 2
                    R.op("act", f_act(rq[:, sb, :], ps(4 + sb), AF.Ln, scale=1.0 / 64, bias=EPS),
                         reads=[("ps", 4 + sb)], writes=[("rq", sb)])
                    R.op("act", f_act(rq[:, sb, :], rq[:, sb, :], AF.Exp, scale=-0.5), reads=[("rq", sb)], writes=[("rq", sb)])
                    dest = (qb if c < 4 else kb_)[:, c % 4, i * TT:(i + 1) * TT]
                    gsc = qkgs[:, 2:3] if c < 4 else qkgs[:, 1:2]
                    R.op("dve", f_stt(dest, ps(pb), gsc, rq[:, sb, :], ALU.mult, ALU.mult),
                         reads=[("ps", pb), ("rq", sb), "qkgs"], writes=[("q" if c < 4 else "k", c % 4, i)])

                nxt = i + 1 < NT
                if nxt:
                    for k in range(KC):
                        norm_sq(k, 1 - b)
                for c in range(10):
                    if c < 8:
                        head(c)
                        if nxt and c < 4:
                            norm_acc(2 * c)
                            norm_acc(2 * c + 1)
                    if 0 <= c - 1 < 8:
                        blkmm(c - 1)
                    if 0 <= c - 2 < 8:
                        epi(c - 2)
                    if nxt and c == 4:
                        norm_rstd()
                    if nxt and 5 <= c < 9:
                        norm_out(2 * (c - 5), 16, 24, 1 - b, 1 - b)
                        norm_out(2 * (c - 5) + 1, 16, 24, 1 - b, 1 - b)
                for sb in range(4):
                    R.mm(ps(sb), [(hb[:, k, sb * 128:(sb + 1) * 128], Wqkv[:, k, 1024:1536]) for k in range(KC)],
                         reads=[("Wqkv", 2)] + HB, writes=[("ps", sb)])
                    R.op("dve", f_copy(vb[:, 4 * i + sb, :], ps(sb)), reads=[("ps", sb)], writes=[("v", 4 * i + sb)])

        def attn_phase():
            its = []
            for pr in range(4):
                for g in range(8):
                    kbs = list(range(4 * g + 3, -1, -1))
                    for idx, kb in enumerate(kbs):
                        its.append((pr, g, kb, idx, idx == 0, kb == 0))
            N = len(its)

            def geom(n):
                pr, g, kb, idx, first, last = its[n]
                j = kb - 4 * g if kb >= 4 * g else None
                c0 = 128 * j if j is not None else 0
                return j, c0

            def zb(n, h2):
                return 2 * (n % 3) + h2

            def S1(n):
                pr, g, kb, idx, first, last = its[n]
                j, c0 = geom(n)
                es = n % 3
                z0 = zb(n, 0)
                for h2 in range(2):
                    rows = slice(h2 * 64, h2 * 64 + 64)
                    R.mm(psum[:, z0 + h2, c0:], [(kb_[rows, pr, kb * 128:(kb + 1) * 128], qb[rows, pr, g * TT + c0:(g + 1) * TT])],
                         reads=[("k", pr, kb // 4), ("q", pr, g)], writes=[("ps", z0 + h2)], start=True, stop=True)
                R.op("act", f_act(e3[es][:, :, c0:], psum[:, z0:z0 + 2, c0:], AF.Exp),
                     reads=[("ps", z0), ("ps", z0 + 1)], writes=[("e", es)])
                R.op("act", f_act(sp3[es][:, :, c0:], e3[es][:, :, c0:], AF.Ln, bias=1.0), reads=[("e", es)], writes=[("sp", es)])
                if j is not None:
                    for h2 in range(2):
                        R.op("dve", f_tt(sp3[es][:, h2, c0:], sp3[es][:, h2, c0:], mask[:, j, c0:], ALU.mult),
                             reads=[("sp", es), "cbf"], writes=[("sp", es)])
                so, sn = n % 3, (n + 1) % 3
                if not last:
                    if first:
                        R.op("dve", f_copy(spsum[sn][:, :, c0:], sp3[es][:, :, c0:]), reads=[("sp", es)], writes=[("spsum", sn)])
                    else:
                        R.op("dve", f_tt(spsum[sn][:, :, c0:], spsum[so][:, :, c0:], sp3[es][:, :, c0:], ALU.add),
                             reads=[("sp", es), ("spsum", so)], writes=[("spsum", sn)])
                    if j is not None and j >= 1:
                        R.op("dve", f_memset(spsum[sn][:, :, c0 - 128:c0], 0.0), writes=[("spsum", sn)])

            def S2(n):
                pr, g, kb, idx, first, last = its[n]
                j, c0 = geom(n)
                es = n % 3
                z0 = zb(n, 0)
                so = n % 3
                for h2 in range(2):
                    R.mm(psum[:, z0 + h2, c0:], [(negtri, sp3[es][:, h2, c0:])], reads=[("sp", es), "cbf"],
                         writes=[("ps", z0 + h2)], start=False, stop=first, skip_group_check=True)
                    if not first:
                        R.mm(psum[:, z0 + h2, c0:], [(negone, spsum[so][:, h2, c0:])], reads=[("spsum", so), "cbf"],
                             writes=[("ps", z0 + h2)], start=False, stop=True, skip_group_check=True)
                R.op("act", f_act(a3[es][:, :, c0:], psum[:, z0:z0 + 2, c0:], AF.Exp),
                     reads=[("ps", z0), ("ps", z0 + 1)], writes=[("a", es)])
                if j is not None:
                    for h2 in range(2):
                        R.op("dve", f_tt(a3[es][:, h2, c0:], a3[es][:, h2, c0:], mask[:, j, c0:], ALU.mult),
                             reads=[("a", es), "cbf"], writes=[("a", es)])

            def S3(n):
                pr, g, kb, idx, first, last = its[n]
                j, c0 = geom(n)
                es = n % 3
                yb = 6 + (pr * 8 + g) % 2
                for h2 in range(2):
                    hh = 2 * pr + h2
                    R.mm(psum[h2 * 64:(h2 + 1) * 64, yb, c0:], [(vb[:, kb, hh * 64:(hh + 1) * 64], a3[es][:, h2, c0:])],
                         reads=[("v", kb), ("a", es)], writes=[("ps", yb, h2)], start=first, stop=last, skip_group_check=True)
                if last:
                    R.op("dve", f_copy(qb[:, pr, g * TT:(g + 1) * TT], ps(yb)),
                         reads=[("ps", yb, 0), ("ps", yb, 1)], writes=[("q", pr, g)])

            for step in range(N + 2):
                if step < N:
                    S1(step)
                if 0 <= step - 1 < N:
                    S2(step - 1)
                if 0 <= step - 2 < N:
                    S3(step - 2)

        def mixout_phase(l):
            win_d = w_in[l].rearrange("(k p) n -> p k n", p=128)
            wo_d = w_out[l].rearrange("(k p) n -> p k n", p=128)
            for s in range(2):
                R.dma("pool", "wg%d" % s, out=Wcp[:, :, s * 512:(s + 1) * 512], in_=win_d[:, :, 1536 + s * 512:1536 + (s + 1) * 512],
                      writes=[("Wcp", s)])
            for s in range(2):
                R.dma("pool", "wu%d" % s, out=Wout[:, :, s * 512:(s + 1) * 512], in_=wo_d[:, :, s * 512:(s + 1) * 512],
                      writes=[("Wout", s)])
            R.dma("pool", "wd0", out=Pw, in_=pwblk[l].rearrange("p (a b) -> p a b", a=2), writes=["Pw"])
            R.op("dve", f_memset(ccu[:, :, 0:2], 0.0), writes=[("ccu", 0), ("ccu", 1)])
            R.op("dve", f_memset(pbuf[:, :, 0:16], 0.0), writes=[("pbuf", 0), ("pbuf", 1)])
            load_x(0, 0)
            load_x(1, 1)
            norm(16, 24, 0, 0)
            for k in range(KC):
                norm_sq(k, 1)
            for i in range(NT):
                b = i % 2
                bx = i % 3
                hb = hs[b]
                HB = H_ALLb(b)
                if i + 2 < NT:
                    load_x(i + 2, (i + 2) % 3)
                nxt = i + 1 < NT
                for pos, c in enumerate((6, 7, 2, 0, 4, 3, 1, 5)):
                    pb = pos % 4
                    R.mm(ps(pb), [(Wcp[:, k, c * 128:(c + 1) * 128], hb[:, k, :]) for k in range(KC)],
                         reads=[("Wcp", c // 4)] + HB, writes=[("ps", pb)])
                    if nxt and pos < 4:
                        norm_acc(2 * pos)
                        norm_acc(2 * pos + 1)
                    if nxt and pos == 4:
                        norm_rstd()
                        if i + 2 < NT:
                            for k in range(KC):
                                norm_sq(k, (i + 2) % 3)
                    if c < 2:
                        R.op("act", f_act(cbs[:, c, :], ps(pb), AF.Copy), reads=[("ps", pb)], writes=[("cbs", c)])
                    elif c < 4:
                        R.op("act", f_act(ccs[:, c - 2, :], ps(pb), AF.Copy), reads=[("ps", pb)], writes=[("ccs", c - 2)])
                    elif c < 6:
                        cc = c - 4
                        R.op("dve", f_tt(ccu[:, cc, 2:514], ps(pb), ccs[:, cc, :], ALU.mult),
                             reads=[("ps", pb), ("ccs", cc)], writes=[("ccu", cc)])
                        R.op("dve", f_ts(acc, ccu[:, cc, 0:512], cws[:, cc * 3:cc * 3 + 1], None, ALU.mult, ALU.bypass),
                             reads=[("ccu", cc), "cws"], writes=["acc"])
                        R.op("dve", f_stt(acc, ccu[:, cc, 1:513], cws[:, cc * 3 + 1:cc * 3 + 2], acc, ALU.mult, ALU.add),
                             reads=[("ccu", cc), "cws", "acc"], writes=["acc"])
                        R.op("dve", f_stt(acc, ccu[:, cc, 2:514], cws[:, cc * 3 + 2:cc * 3 + 3], acc, ALU.mult, ALU.add),
                             reads=[("ccu", cc), "cws", "acc"], writes=["acc"])
                        R.op("dve", f_tt(ymix[:, cc, :], acc, cbs[:, cc, :], ALU.mult),
                             reads=["acc", ("cbs", cc)], writes=[("ymix", cc)])
                        R.op("dve", f_copy(ccu[:, cc, 0:2], ccu[:, cc, 512:514]), reads=[("ccu", cc)], writes=[("ccu", cc)])
                    else:
                        pc = c - 6
                        pooled = pooled2[:, pc, :]
                        R.op("act", f_act(pbuf[:, pc, 16:528], ps(pb), AF.Copy), reads=[("ps", pb)], writes=[("pbuf", pc)])
                        P = pbuf[:, pc, :]
                        R.op("dve", f_tt(s2[:, 1:528], P[:, 1:528], P[:, 0:527], ALU.add), reads=[("pbuf", pc)], writes=["s2"])
                        lo, hi = slice(0, 64), slice(64, 128)
                        if pc == 0:
                            R.op("dve", f_tt(s4[hi, 3:528], s2[hi, 3:528], s2[hi, 1:526], ALU.add), reads=["s2"], writes=["s4"])
                            Slo, Shi = s2, s4
                        else:
                            R.op("dve", f_tt(s4[:, 3:528], s2[:, 3:528], s2[:, 1:526], ALU.add), reads=["s2"], writes=["s4"])
                            R.op("dve", f_tt(s8[:, 7:528], s4[:, 7:528], s4[:, 3:524], ALU.add), reads=["s4"], writes=["s8"])
                            R.op("dve", f_tt(s16[hi, 15:528], s8[hi, 15:528], s8[hi, 7:520], ALU.add), reads=["s8"], writes=["s16"])
                            Slo, Shi = s8, s16
                        rd = [("pbuf", pc), "s2", "s4", "s8", "s16", "cf32"]
                        R.op("dve", f_stt(pooled[lo, :], Slo[lo, 16:528], invw[lo, pc:pc + 1], P[lo, 16:528], ALU.mult, ALU.subtract),
                             reads=rd, writes=[("pooled", pc)])
                        R.op("dve", f_stt(pooled[hi, :], Shi[hi, 16:528], invw[hi, pc:pc + 1], P[hi, 16:528], ALU.mult, ALU.subtract),
                             reads=rd, writes=[("pooled", pc)])
                        if i == 0:
                            for half, S_ in ((lo, Slo), (hi, Shi)):
                                R.op("dve", f_tt(acc[half, 0:16], S_[half, 16:32], invcnt[half, pc, :], ALU.mult),
                                     reads=rd, writes=["acc"])
                                R.op("dve", f_tt(pooled[half, 0:16], acc[half, 0:16], P[half, 16:32], ALU.subtract),
                                     reads=rd + ["acc"], writes=[("pooled", pc)])
                        R.op("dve", f_copy(pbuf[:, pc, 0:16], pbuf[:, pc, 512:528]), reads=[("pbuf", pc)], writes=[("pbuf", pc)])
                for pc in range(2):
                    R.mm(ps(pc), [(Pw[:, pc, :], pooled2[:, pc, :])], reads=["Pw", ("pooled", pc)], writes=[("ps", pc)])
                    R.op("act", f_act(ymix[:, 2 + pc, :], ps(pc), AF.Identity, scale=pscs[:, pc:pc + 1]),
                         reads=[("ps", pc), "pscs"], writes=[("ymix", 2 + pc)])
                for dk in range(KC):
                    ob = 4 + dk % 2
                    pairs = []
                    for mk in range(8):
                        rhs = qb[:, mk, i * TT:(i + 1) * TT] if mk < 4 else ymix[:, mk - 4, :]
                        pairs.append((Wout[:, mk, dk * 128:(dk + 1) * 128], rhs))
                    R.mm(ps(ob), pairs, reads=[("Wout", dk // 4)] + [("ymix", m) for m in range(4)] + [("q", m, i) for m in range(4)],
                         writes=[("ps", ob)])
                    xbi = dk % 2
                    txb = (s2b, s4b)[xbi][:, 0:512]
                    R.op("act", f_act(txb, ps(ob), AF.Identity, scale=AG[:, 24 + dk:24 + dk + 1]),
                         reads=[("ps", ob), "AG"], writes=[("tx", xbi)])
                    R.op("pool", f_tt(xsb[:, xbi, :], txb, xts[bx][:, dk, :], ALU.add),
                         reads=[("tx", xbi), ("xt", bx)], writes=[("xsb", xbi)])
                    R.dma("sp", "xst%d" % xbi, out=xs_d[dk, :, i * TT:(i + 1) * TT], in_=xsb[:, xbi, :], reads=[("xsb", xbi)],
                          writes=[xname(i, dk)])
                    if nxt:
                        norm_out(dk, 16, 24, (i + 1) % 3, 1 - b)

        phases = []
        for l in range(n_layers):
            phases += [("ada", l), ("ffn1", l), ("proj", l), ("attn", l), ("mixout", l), ("ffn2", l)]
        if stop_after is not None:
            phases = phases[:phases.index(tuple(stop_after)) + 1]
        final = phases[-1]
        for ph in phases:
            name, l = ph
            R.barrier()
            if name == "ada":
                ada_phase(l)
            elif name == "ffn1":
                ffn_phase(l, 1, xs_d, True)
                state["src"], state["src_is_scr"] = xs_d, True
            elif name == "proj":
                proj_phase(l)
            elif name == "attn":
                attn_phase()
            elif name == "mixout":
                mixout_phase(l)
            elif name == "ffn2":
                if ph == final and stop_after is None:
                    ffn_phase(l, 2, yT, False)
                else:
                    ffn_phase(l, 2, xs_d, True)
        R.barrier()
        if stop_after is not None:
            for dk in range(KC):
                R.dma("sp", "xst0", out=yT[dk], in_=xs_d[dk])
            qd = nc.dram_tensor("qdump", [128, 24576], F32, kind="ExternalOutput").ap()
            R.dma("sp", "xst0", out=qd, in_=arena[:, 0:24576])
            md = nc.dram_tensor("mdump", [128, 120], F32, kind="ExternalOutput").ap()
            R.dma("sp", "xst0", out=md[:, 0:72], in_=modv)
            R.dma("sp", "xst0", out=md[:, 72:120], in_=AG)
            R.barrier()

        block = es.enter_context(nc.Block())

        @block.tensor
        def _(e):
            for f in R.q["pe"]:
                f(e)

        @block.scalar
        def _(e):
            for f in R.q["act"]:
                f(e)

        @block.vector
        def _(e):
            for f in R.q["dve"]:
                f(e)

        @block.gpsimd
        def _(e):
            for f in R.q["pool"]:
                f(e)

        @block.sync
        def _(e):
            for f in R.q["sp"]:
                f(e)

    return nc, R.ninstr


def make_consts():
    c = np.zeros((128, NCST), np.float32)
    j = np.arange(128)[:, None]
    s = np.arange(128)[None, :]
    c[:, C_NEGTRI:C_NEGTRI + 128] = -(j >= s).astype(np.float32)
    c[:, C_NEGONE:C_NEGONE + 128] = -1.0
    c[:, C_BLK:C_BLK + 128] = ((j // 64) == (s // 64)).astype(np.float32)
    c[:, C_ONES:C_ONES + 128] = 1.0
    t = np.arange(512)[None, :]
    for jj in range(4):
        c[:, C_MASK + jj * 512:C_MASK + (jj + 1) * 512] = (t > 128 * jj + j).astype(np.float32)
    wins = (2, 4, 8, 16)
    for pc in range(2):
        for half in range(2):
            w = wins[pc * 2 + half]
            rows = slice(half * 64, half * 64 + 64)
            c[rows, C_INVCNT + pc * 16:C_INVCNT + (pc + 1) * 16] = 1.0 / np.minimum(np.arange(1, 17), w)
            c[rows, C_INVW + pc] = 1.0 / w
    return c


def prep_shared(inp):
    f = lambda a: np.ascontiguousarray(np.asarray(a, dtype=np.float32))
    sh = {}
    sh["cst"] = make_consts()
    sh["w_ada"] = f(inp["w_ada"])
    sh["b_adaT"] = f(np.asarray(inp["b_ada"]).reshape(L, 72, 128).transpose(0, 2, 1))
    nr = np.stack([np.asarray(inp["ffn1_norm"]), np.asarray(inp["mix_norm"]), np.asarray(inp["ffn2_norm"])], axis=1)
    sh["norms"] = f(nr.reshape(L, 3, 8, 128).transpose(0, 3, 1, 2).reshape(L, 128, 24))
    for k in ("ffn1_gate", "ffn1_up", "ffn1_down", "ffn2_gate", "ffn2_up", "ffn2_down", "w_in", "w_out"):
        sh[k] = f(inp[k])
    qn = np.asarray(inp["q_norm"])
    kn = np.asarray(inp["k_norm"])
    sh["qkg"] = f(np.stack([np.tile(qn, (1, 2)), np.tile(kn, (1, 2))], axis=2))
    cw = np.asarray(inp["conv_w"])
    sh["convw"] = f(cw.reshape(L, 3, 2, 128).transpose(0, 3, 2, 1).reshape(L, 128, 6))
    pw = np.asarray(inp["pool_w"])
    blkm = np.zeros((L, 128, 2, 128), np.float32)
    for pc in range(2):
        for half in range(2):
            blkm[:, half * 64:(half + 1) * 64, pc, half * 64:(half + 1) * 64] = pw[:, pc * 2 + half]
    sh["pwblk"] = f(blkm.reshape(L, 128, 256))
    sh["pscale"] = f(np.asarray(inp["pool_scale"]).reshape(L, 2, 128).transpose(0, 2, 1))
    return sh


def prep_core(inp, b):
    x = np.asarray(inp["x"][b], dtype=np.float32)
    xT = np.ascontiguousarray(x.T).reshape(KC, 128, T)
    c = np.asarray(inp["c"][b], dtype=np.float32)
    return {"xT": xT, "ccol": np.ascontiguousarray(c.reshape(KC, 128).T)}


_CACHE = {}


def kernel(**inputs):
    if "nc" not in _CACHE:
        _CACHE["nc"] = build()[0]
    nc = _CACHE["nc"]
    sh = prep_shared(inputs)
    B = np.asarray(inputs["x"]).shape[0]
    in_maps = []
    for b in range(B):
        m = dict(sh)
        m.update(prep_core(inputs, b))
        in_maps.append(m)
    res = run_bass_kernel_spmd(nc, in_maps, core_ids=list(range(B)))
    out = np.empty((B, T, D), np.float32)
    for b in range(B):
        out[b] = res.results[b]["yT"].reshape(D, T).T
    return out
```

```python
import numpy as np
import concourse.bass as bass
import concourse.mybir as mybir
from concourse.bass_utils import run_bass_kernel_spmd
from contextlib import ExitStack

F32 = mybir.dt.float32
BF16 = mybir.dt.bfloat16
AF = mybir.ActivationFunctionType
ALU = mybir.AluOpType

L = 4
D = 1024
T = 4096
FF = 2816
NT = 8
TT = 512
KC = 8
FC = 22
EPS = 1e-6
ENGS = ("pe", "act", "dve", "pool", "sp")
WQ2 = "pool"

C_NEGTRI, C_NEGONE, C_BLK, C_ONES, C_MASK = 0, 128, 256, 384, 512
C_BF_END = 512 + 4 * 512
C_INVCNT = C_BF_END
C_INVW = C_INVCNT + 32
NCST = C_INVW + 2


class Rec:
    def __init__(self, nc, es):
        self.nc = nc
        self.es = es
        self.q = {e: [] for e in ENGS}
        self.sem = {}
        self.cnt = {}
        for e in ("pe", "act", "dve", "pool"):
            self.sem[e] = es.enter_context(nc.semaphore("s_" + e))
            self.cnt[e] = 0
        self.seen = {e: {} for e in ENGS}
        self.w = {}
        self.r = {}
        self.ninstr = 0

    def chan(self, name):
        self.sem[name] = self.es.enter_context(self.nc.semaphore("d_" + name))
        self.cnt[name] = 0
        return name

    def _deps(self, reads, writes):
        deps = []
        for b in reads:
            if b in self.w:
                deps.append(self.w[b])
        for b in writes:
            if b in self.w:
                deps.append(self.w[b])
            deps.extend(self.r.get(b, ()))
        return deps

    def _commit(self, tok, reads, writes):
        if tok is None:
            return
        for b in reads:
            self.r.setdefault(b, []).append(tok)
        for b in writes:
            self.w[b] = tok
            self.r[b] = []

    def _wait(self, eng, deps):
        best = {}
        for key, val in deps:
            if eng == "pe" and key == "pe":
                continue
            if val > best.get(key, 0):
                best[key] = val
        for key, val in best.items():
            if self.seen[eng].get(key, 0) >= val:
                continue
            self.seen[eng][key] = val
            sem = self.sem[key]
            self.q[eng].append(lambda e, sem=sem, val=val: e.wait_ge(sem, val))

    def op(self, eng, fn, reads=(), writes=(), sig=True):
        self._wait(eng, self._deps(reads, writes))
        self.ninstr += 1
        if sig:
            self.cnt[eng] += 1
            sem = self.sem[eng]
            self.q[eng].append(lambda e, fn=fn, sem=sem: fn(e).then_inc(sem, 1))
            tok = (eng, self.cnt[eng])
        else:
            self.q[eng].append(fn)
            tok = None
        self._commit(tok, reads, writes)
        return tok

    def mm(self, out, pairs, reads=(), writes=(), start=True, stop=True, **kw):
        self._wait("pe", self._deps(reads, writes))
        n = len(pairs)
        for i, (lhsT, rhs) in enumerate(pairs):
            st = start and i == 0
            sp = stop and i == n - 1
            self.ninstr += 1
            if i == n - 1:
                self.cnt["pe"] += 1
                sem = self.sem["pe"]
                self.q["pe"].append(lambda e, out=out, lhsT=lhsT, rhs=rhs, st=st, sp=sp, sem=sem, kw=kw:
                                    e.matmul(out, lhsT=lhsT, rhs=rhs, start=st, stop=sp, **kw).then_inc(sem, 1))
            else:
                self.q["pe"].append(lambda e, out=out, lhsT=lhsT, rhs=rhs, st=st, sp=sp, kw=kw:
                                    e.matmul(out, lhsT=lhsT, rhs=rhs, start=st, stop=sp, **kw))
        tok = ("pe", self.cnt["pe"])
        self._commit(tok, reads, writes)
        return tok

    def dma(self, qeng, ch, out, in_, reads=(), writes=()):
        self._wait(qeng, self._deps(reads, writes))
        self.ninstr += 1
        self.cnt[ch] += 16
        sem = self.sem[ch]
        self.q[qeng].append(lambda e, out=out, in_=in_, sem=sem: e.dma_start(out=out, in_=in_).then_inc(sem, 16))
        tok = (ch, self.cnt[ch])
        self._commit(tok, reads, writes)
        return tok

    def group(self, names):
        last = max((self.w[b] for b in names), key=lambda t: t[1])
        for b in names:
            assert self.w[b][0] == last[0]
            self.w[b] = last

    def barrier(self):
        allk = [(k, v) for k, v in self.cnt.items() if v > 0]
        for e in ENGS:
            self._wait(e, [(k, v) for k, v in allk])
        self.w.clear()
        self.r.clear()


def f_act(out, in_, func, **kw):
    return lambda e: e.activation(out=out, in_=in_, func=func, **kw)


def f_tt(out, in0, in1, op):
    return lambda e: e.tensor_tensor(out=out, in0=in0, in1=in1, op=op)


def f_ts(out, in0, s1, s2, op0, op1):
    return lambda e: e.tensor_scalar(out=out, in0=in0, scalar1=s1, scalar2=s2, op0=op0, op1=op1)


def f_stt(out, in0, scalar, in1, op0, op1):
    return lambda e: e.scalar_tensor_tensor(out=out, in0=in0, scalar=scalar, in1=in1, op0=op0, op1=op1)


def f_copy(out, in_):
    return lambda e: e.tensor_copy(out=out, in_=in_)


def f_recip(out, in_):
    return lambda e: e.reciprocal(out=out, in_=in_)


def f_memset(ap, v):
    return lambda e: e.memset(ap, v)


def actb_xt2(view, o_actb):
    return view(o_actb, [KC, 512], F32)


def build(n_layers=L, stop_after=None):
    nc = bass.Bass("TRN2", target_bir_lowering=False)

    def din(name, shape):
        return nc.dram_tensor(name, shape, F32, kind="ExternalInput").ap()

    xT = din("xT", [KC, 128, T])
    ccol = din("ccol", [128, KC])
    cst = din("cst", [128, NCST])
    w_ada = din("w_ada", [L, D, 9 * D])
    b_adaT = din("b_adaT", [L, 128, 72])
    norms = din("norms", [L, 128, 24])
    fgate = [din("ffn1_gate", [L, D, FF]), din("ffn2_gate", [L, D, FF])]
    fup = [din("ffn1_up", [L, D, FF]), din("ffn2_up", [L, D, FF])]
    fdown = [din("ffn1_down", [L, FF, D]), din("ffn2_down", [L, FF, D])]
    w_in = din("w_in", [L, D, 2560])
    w_out = din("w_out", [L, D, D])
    qkg = din("qkg", [L, 128, 2])
    convw = din("convw", [L, 128, 6])
    pwblk = din("pwblk", [L, 128, 256])
    pscale = din("pscale", [L, 128, 2])
    yT = nc.dram_tensor("yT", [KC, 128, T], F32, kind="ExternalOutput").ap()
    xs_d = nc.dram_tensor("xscr", [KC, 128, T], F32, kind="Internal").ap()

    with ExitStack() as es:
        ARENA_B = 212480
        arena = es.enter_context(nc.sbuf_tensor("arena", [128, ARENA_B // 4], F32))
        psum = es.enter_context(nc.psum_tensor("ps", [128, 8, 512], F32))
        R = Rec(nc, es)
        for ch in ["wg%d" % j for j in range(6)] + ["wu%d" % j for j in range(6)] + ["wd%d" % j for j in range(6)] + ["wa0", "wa1", "xl0", "xl1", "xl2", "xs0", "xs1", "xst0", "xst1", "misc", "miscp"]:
            R.chan(ch)

        def view(off, shape, dtype):
            n = int(np.prod(shape))
            esz = 4 if dtype == F32 else 2
            assert off % 4 == 0 and (n * esz) % 4 == 0
            a = arena[:, off // 4:(off + n * esz) // 4]
            if dtype != F32:
                a = a.bitcast(dtype)
            if len(shape) == 2:
                a = a.rearrange("p (a b) -> p a b", a=shape[0])
            return a

        KB = 1024
        BIG = 0
        Wg = view(BIG, [KC, FF], BF16)
        Wu = view(BIG + 44 * KB, [KC, FF], BF16)
        Wd = view(BIG + 88 * KB, [FC, D], BF16)
        wa = [view(BIG + 88 * KB + s * 16 * KB, [KC, 1024], BF16) for s in range(2)]
        qb = view(BIG, [4, T], BF16)
        kb_ = view(BIG + 32 * KB, [4, T], BF16)
        vb = view(BIG + 64 * KB, [32, 512], BF16)
        Wqkv = view(BIG + 96 * KB, [KC, 1536], BF16)
        Wcp = view(BIG + 32 * KB, [KC, 1024], BF16)
        Wout = view(BIG + 48 * KB, [KC, 1024], BF16)
        Pw = view(BIG + 64 * KB, [2, 128], BF16)
        mo = BIG + 66 * KB
        cbs = view(mo, [2, 512], F32); mo += 4 * KB
        ccs = view(mo, [2, 512], F32); mo += 4 * KB
        ccu = view(mo, [2, 514], F32); mo += 2 * 514 * 4
        acc = view(mo, [512], F32); mo += 2 * KB
        pbuf = view(mo, [2, 528], F32); mo += 2 * 528 * 4
        s2 = view(mo, [528], F32); mo += 528 * 4
        s4 = view(mo, [528], F32); mo += 528 * 4
        s8 = view(mo, [528], F32); mo += 528 * 4
        s16 = view(mo, [528], F32); mo += 528 * 4
        pooled2 = view(mo, [2, 512], BF16); mo += 2 * KB
        s2b = view(mo, [528], F32); mo += 528 * 4
        s4b = view(mo, [528], F32); mo += 528 * 4
        ymix = view(mo, [4, 512], BF16); mo += 4 * KB
        assert mo <= 104 * KB, mo
        o = 132 * KB
        xt = view(o, [KC, 512], F32); o += 16 * KB
        xsb = view(o, [2, 512], F32); o += 4 * KB
        h = view(o, [KC, 512], BF16); o += 8 * KB
        o_actb = o
        actb = view(o, [FC, 512], BF16); o += 22 * KB
        sqk = view(o, [2, 512], BF16); o += 2 * KB
        tmp = view(o, [2, 512], F32); o += 4 * KB
        rt = view(o, [512], F32); o += 2 * KB
        rstd = view(o, [512], F32); o += 2 * KB
        sg = view(o, [2, 512], BF16); o += 2 * KB
        sq8 = [actb[:, 16 + j, :] for j in range(6)] + [sqk[:, 0, :], sqk[:, 1, :]]
        xts = [xt, actb_xt2(view, o_actb), view(BIG + 104 * KB, [KC, 512], F32)]
        hs = [h, view(BIG + 120 * KB, [KC, 512], BF16)]
        cbf = view(o, [C_BF_END], BF16); o += C_BF_END * 2
        cf32 = view(o, [34], F32); o += 34 * 4
        modv = view(o, [72], F32); o += 72 * 4
        bada = view(o, [72], F32); o += 72 * 4
        nrm = view(o, [24], F32); o += 24 * 4
        AG = view(o, [48], F32); o += 48 * 4
        qkgs = view(o, [4], F32); o += 16
        cws = view(o, [6], F32); o += 24
        pscs = view(o, [2], F32); o += 8
        ccs_ = view(o, [KC], F32); o += 32
        cond = view(o, [KC], BF16); o += 16
        assert o <= ARENA_B, o

        negtri = cbf[:, C_NEGTRI:C_NEGTRI + 128]
        negone = cbf[:, C_NEGONE:C_NEGONE + 128]
        blk = cbf[:, C_BLK:C_BLK + 128]
        ones = cbf[:, C_ONES:C_ONES + 128]
        mask = cbf[:, C_MASK:C_MASK + 2048].rearrange("p (a b) -> p a b", a=4)
        invcnt = cf32[:, 0:32].rearrange("p (a b) -> p a b", a=2)
        invw = cf32[:, 32:34]
        e3 = [xt[:, 2 * i:2 * i + 2, :] for i in range(3)]
        sp3 = [actb[:, 2 * i:2 * i + 2, :] for i in range(3)]
        a3 = [actb[:, 6 + 2 * i:8 + 2 * i, :] for i in range(3)]
        spsum = [actb[:, 12 + 2 * i:14 + 2 * i, :] for i in range(3)]

        def ps(b):
            return psum[:, b, :]

        R.dma("pool", "miscp", out=cbf[:, 0:1280], in_=cst[:, 0:1280], writes=["cbf0"])
        R.dma("pool", "miscp", out=cbf[:, 1280:C_BF_END], in_=cst[:, 1280:C_BF_END], writes=["cbf"])
        R.dma("sp", "misc", out=cf32, in_=cst[:, C_BF_END:NCST], writes=["cf32"])
        R.dma("sp", "misc", out=ccs_, in_=ccol, writes=["ccol"])
        R.group(["cbf0", "cbf"])
        R.group(["cf32", "ccol"])
        R.op("act", f_act(cond, ccs_, AF.Silu), reads=["ccol"], writes=["cond"])

        state = {"src": xT, "src_is_scr": False}

        def xname(i, dk):
            return ("X", i, dk)

        def load_x(i, b=0):
            src = state["src"]
            rd = [xname(i, dk) for dk in range(KC)] if state["src_is_scr"] else []
            R.dma("sp", "xl%d" % b, out=xts[b], in_=src.rearrange("k p t -> p k t")[:, :, i * TT:(i + 1) * TT],
                  reads=rd, writes=[("xt", b)])

        def norm_head(k, bx):
            sb = k % 2
            xb = xts[bx]
            R.op("act", f_act(sqk[:, sb, :], xb[:, k, :], AF.Square), reads=[("xt", bx)], writes=[("sqk", sb)])
            R.mm(ps(6), [(ones, sqk[:, sb, :])], reads=[("sqk", sb), "cbf"], writes=[("ps", 6)],
                 start=(k == 0), stop=(k == KC - 1))

        def norm_tail(Aoff, Boff, bx, b):
            xb, hb = xts[bx], hs[b]
            R.op("act", f_act(rt, ps(6), AF.Ln, scale=1.0 / D, bias=EPS), reads=[("ps", 6)], writes=["rt"])
            R.op("act", f_act(rstd, rt, AF.Exp, scale=-0.5), reads=["rt"], writes=["rstd"])
            for k in range(KC):
                tb = k % 2
                R.op("dve", f_stt(tmp[:, tb, :], xb[:, k, :], AG[:, Aoff + k:Aoff + k + 1], rstd, ALU.mult, ALU.mult),
                     reads=[("xt", bx), "rstd", "AG"], writes=[("tmp", tb)])
                R.op("act", f_act(hb[:, k, :], tmp[:, tb, :], AF.Identity, bias=modv[:, Boff + k:Boff + k + 1], scale=1.0),
                     reads=[("tmp", tb), "modv"], writes=[("h", b, k)])

        def norm(Aoff, Boff, b=0, bh=None):
            bx = b
            b = bx if bh is None else bh
            for k in range(KC):
                norm_head(k, bx)
            norm_tail(Aoff, Boff, bx, b)

        def norm_sq(k, bx):
            R.op("pool", f_tt(sq8[k], xts[bx][:, k, :], xts[bx][:, k, :], ALU.mult), reads=[("xt", bx)], writes=[("sq8", k) if k < 6 else ("sqk", k - 6)])

        def norm_acc(k):
            R.mm(ps(6), [(ones, sq8[k])], reads=[("sq8", k) if k < 6 else ("sqk", k - 6), "cbf"], writes=[("ps", 6)],
                 start=(k == 0), stop=(k == KC - 1))

        def norm_rstd():
            R.op("act", f_act(rt, ps(6), AF.Ln, scale=1.0 / D, bias=EPS), reads=[("ps", 6)], writes=["rt"])
            R.op("act", f_act(rstd, rt, AF.Exp, scale=-0.5), reads=["rt"], writes=["rstd"])

        def norm_out(k, Aoff, Boff, bx, bh):
            tb = k % 2
            R.op("dve", f_stt(tmp[:, tb, :], xts[bx][:, k, :], AG[:, Aoff + k:Aoff + k + 1], rstd, ALU.mult, ALU.mult),
                 reads=[("xt", bx), "rstd", "AG"], writes=[("tmp", tb)])
            R.op("pool", f_ts(hs[bh][:, k, :], tmp[:, tb, :], 1.0, modv[:, Boff + k:Boff + k + 1], ALU.mult, ALU.add),
                 reads=[("tmp", tb), "modv"], writes=[("h", bh, k)])

        def H_ALLb(b):
            return [("h", b, k) for k in range(KC)]

        H_ALL = H_ALLb(0)

        def xupdate(i, dk, ob, Goff, dst, dst_is_scr):
            src = state["src"]
            xbi = dk % 2
            rd = [xname(i, dk)] if state["src_is_scr"] else []
            R.dma("sp", "xs%d" % xbi, out=xsb[:, xbi, :], in_=src[dk, :, i * TT:(i + 1) * TT], reads=rd, writes=[("xsb", xbi)])
            R.op("dve", f_stt(xsb[:, xbi, :], ps(ob), AG[:, Goff + dk:Goff + dk + 1], xsb[:, xbi, :], ALU.mult, ALU.add),
                 reads=[("ps", ob), ("xsb", xbi), "AG"], writes=[("xsb", xbi)])
            wr = [xname(i, dk)] if dst_is_scr else [("Y", i, dk)]
            R.dma("sp", "xst%d" % xbi, out=dst[dk, :, i * TT:(i + 1) * TT], in_=xsb[:, xbi, :], reads=[("xsb", xbi)], writes=wr)

        def ada_phase(l):
            R.dma("sp", "misc", out=bada, in_=b_adaT[l], writes=["bada"])
            R.dma("sp", "misc", out=nrm, in_=norms[l], writes=["nrm"])
            R.dma("sp", "misc", out=qkgs[:, 0:2], in_=qkg[l], writes=["qkgs"])
            R.dma("sp", "misc", out=cws, in_=convw[l], writes=["cws"])
            R.dma("sp", "misc", out=pscs, in_=pscale[l], writes=["pscs"])
            R.group(["bada", "nrm", "qkgs", "cws", "pscs"])
            wa_d = w_ada[l].rearrange("(k p) n -> p k n", p=128)
            for s in range(9):
                slot = s % 2
                R.dma("pool" if s % 2 == 0 else WQ2, "wa%d" % slot, out=wa[slot], in_=wa_d[:, :, s * 1024:(s + 1) * 1024], writes=[("wa", slot)])
                for j in range(8):
                    n = s * 8 + j
                    R.mm(psum[:, 7, n:n + 1], [(wa[slot][:, k, j * 128:(j + 1) * 128], cond[:, k:k + 1]) for k in range(KC)],
                         reads=[("wa", slot), "cond"], writes=[("ps", 7)])
            R.op("dve", f_tt(modv, psum[:, 7, 0:72], bada, ALU.add), reads=[("ps", 7), "bada"], writes=["modv"])
            for t in range(3):
                sc = modv[:, (3 * t + 1) * 8:(3 * t + 2) * 8]
                g = modv[:, (3 * t + 2) * 8:(3 * t + 3) * 8]
                R.op("dve", f_stt(AG[:, t * 16:t * 16 + 8], sc, 1.0, nrm[:, t * 8:(t + 1) * 8], ALU.add, ALU.mult),
                     reads=["modv", "nrm"], writes=["AG"])
                R.op("dve", f_ts(AG[:, t * 16 + 8:t * 16 + 16], g, 1.0, (1.0 if t == 1 else 0.5), ALU.add, ALU.mult),
                     reads=["modv"], writes=["AG"])
            R.op("dve", f_ts(qkgs[:, 2:3], qkgs[:, 0:1], 0.125, None, ALU.mult, ALU.bypass), reads=["qkgs"], writes=["qkgs"])

        def ffn_phase(l, which, dst, dst_is_scr):
            wi = which - 1
            t3 = 0 if which == 1 else 2
            Aoff, Boff, Goff = t3 * 16, (3 * t3) * 8, t3 * 16 + 8
            g_d = fgate[wi][l].rearrange("(k p) f -> p k f", p=128)
            u_d = fup[wi][l].rearrange("(k p) f -> p k f", p=128)
            d_d = fdown[wi][l].rearrange("(c p) d -> p c d", p=128)
            load_x(0)
            for s in range(6):
                c0, c1 = s * 512, min(FF, s * 512 + 512)
                R.dma("pool", "wg%d" % s, out=Wg[:, :, c0:c1], in_=g_d[:, :, c0:c1], writes=[("Wg", s)])
                R.dma(WQ2, "wu%d" % s, out=Wu[:, :, c0:c1], in_=u_d[:, :, c0:c1], writes=[("Wu", s)])
            for s in range(6):
                f0, f1 = s * 4, min(FC, s * 4 + 4)
                R.dma("pool", "wd%d" % s, out=Wd[:, f0:f1, :], in_=d_d[:, f0:f1, :], writes=[("Wd", s)])

            def gateup(i, nxt=False):
                for fc in range(FC):
                    if nxt and 10 <= fc < 18:
                        norm_head(fc - 10, 0)
                    gbk = fc % 2
                    sl = slice(fc * 128, (fc + 1) * 128)
                    R.mm(ps(gbk), [(Wg[:, k, sl], h[:, k, :]) for k in range(KC)],
                         reads=[("Wg", fc // 4)] + H_ALL, writes=[("ps", gbk)])
                    R.mm(ps(2 + gbk), [(Wu[:, k, sl], h[:, k, :]) for k in range(KC)],
                         reads=[("Wu", fc // 4)] + H_ALL, writes=[("ps", 2 + gbk)])
                    R.op("act", f_act(sg[:, gbk, :], ps(gbk), AF.Silu), reads=[("ps", gbk)], writes=[("sg", gbk)])
                    R.op("dve", f_tt(actb[:, fc, :], ps(2 + gbk), sg[:, gbk, :], ALU.mult),
                         reads=[("ps", 2 + gbk), ("sg", gbk)], writes=[("act", fc)])

            def down(i):
                for dk in range(KC):
                    ob = 4 + dk % 2
                    R.mm(ps(ob), [(Wd[:, fc, dk * 128:(dk + 1) * 128], actb[:, fc, :]) for fc in range(FC)],
                         reads=[("Wd", s) for s in range(6)] + [("act", fc) for fc in range(FC)], writes=[("ps", ob)])
                    xupdate(i, dk, ob, Goff, dst, dst_is_scr)

            norm(Aoff, Boff)
            for i in range(NT):
                nxt = i + 1 < NT
                if nxt:
                    load_x(i + 1)
                gateup(i, nxt)
                if nxt:
                    norm_tail(Aoff, Boff, 0, 0)
                down(i)

        def proj_phase(l):
            win_d = w_in[l].rearrange("(k p) n -> p k n", p=128)
            for s in range(3):
                R.dma("pool", "wg%d" % s, out=Wqkv[:, :, s * 512:(s + 1) * 512], in_=win_d[:, :, s * 512:(s + 1) * 512],
                      writes=[("Wqkv", s)])
            rq = xsb
            load_x(0, 0)
            load_x(1, 1)
            norm(16, 24, 0)
            for i in range(NT):
                b = i % 2
                hb = hs[b]
                HB = H_ALLb(b)
                if i + 2 < NT:
                    load_x(i + 2, b)

                def head(c):
                    pb = c % 4
                    sb = c % 2
                    R.mm(ps(pb), [(Wqkv[:, k, c * 128:(c + 1) * 128], hb[:, k, :]) for k in range(KC)],
                         reads=[("Wqkv", c // 4)] + HB, writes=[("ps", pb)])
                    R.op("act", f_act(sg[:, sb, :], ps(pb), AF.Square), reads=[("ps", pb)], writes=[("sg", sb)])

                def blkmm(c):
                    sb = c % 2
                    R.mm(ps(4 + sb), [(blk, sg[:, sb, :])], reads=[("sg", sb), "cbf"], writes=[("ps", 4 + sb)])

                def epi(c):
                    pb = c % 4
                    sb = c % 2
                    R.op("act", f_act(rq[:, sb, :], ps(4 + sb), AF.Ln, scale=1.0 / 64, bias=EPS),
                         reads=[("ps", 4 + sb)], writes=[("rq", sb)])
                    R.op("act", f_act(rq[:, sb, :], rq[:, sb, :], AF.Exp, scale=-0.5), reads=[("rq", sb)], writes=[("rq", sb)])
                    dest = (qb if c < 4 else kb_)[:, c % 4, i * TT:(i + 1) * TT]
                    gsc = qkgs[:, 2:3] if c < 4 else qkgs[:, 1:2]
                    R.op("dve", f_stt(dest, ps(pb), gsc, rq[:, sb, :], ALU.mult, ALU.mult),
                         reads=[("ps", pb), ("rq", sb), "qkgs"], writes=[("q" if c < 4 else "k", c % 4, i)])

                nxt = i + 1 < NT
                if nxt:
                    for k in range(KC):
                        norm_sq(k, 1 - b)
                for c in range(10):
                    if c < 8:
                        head(c)
                        if nxt and c < 4:
                            norm_acc(2 * c)
                            norm_acc(2 * c + 1)
                    if 0 <= c - 1 < 8:
                        blkmm(c - 1)
                    if 0 <= c - 2 < 8:
                        epi(c - 2)
                    if nxt and c == 4:
                        norm_rstd()
                    if nxt and 5 <= c < 9:
                        norm_out(2 * (c - 5), 16, 24, 1 - b, 1 - b)
                        norm_out(2 * (c - 5) + 1, 16, 24, 1 - b, 1 - b)
                for sb in range(4):
                    R.mm(ps(sb), [(hb[:, k, sb * 128:(sb + 1) * 128], Wqkv[:, k, 1024:1536]) for k in range(KC)],
                         reads=[("Wqkv", 2)] + HB, writes=[("ps", sb)])
                    R.op("dve", f_copy(vb[:, 4 * i + sb, :], ps(sb)), reads=[("ps", sb)], writes=[("v", 4 * i + sb)])

        def attn_phase():
            its = []
            for pr in range(4):
                for g in range(8):
                    kbs = list(range(4 * g + 3, -1, -1))
                    for idx, kb in enumerate(kbs):
                        its.append((pr, g, kb, idx, idx == 0, kb == 0))
            N = len(its)

            def geom(n):
                pr, g, kb, idx, first, last = its[n]
                j = kb - 4 * g if kb >= 4 * g else None
                c0 = 128 * j if j is not None else 0
                return j, c0

            def zb(n, h2):
                return 2 * (n % 3) + h2

            def S1(n):
                pr, g, kb, idx, first, last = its[n]
                j, c0 = geom(n)
                es = n % 3
                z0 = zb(n, 0)
                for h2 in range(2):
                    rows = slice(h2 * 64, h2 * 64 + 64)
                    R.mm(psum[:, z0 + h2, c0:], [(kb_[rows, pr, kb * 128:(kb + 1) * 128], qb[rows, pr, g * TT + c0:(g + 1) * TT])],
                         reads=[("k", pr, kb // 4), ("q", pr, g)], writes=[("ps", z0 + h2)], start=True, stop=True)
                R.op("act", f_act(e3[es][:, :, c0:], psum[:, z0:z0 + 2, c0:], AF.Exp),
                     reads=[("ps", z0), ("ps", z0 + 1)], writes=[("e", es)])
                R.op("act", f_act(sp3[es][:, :, c0:], e3[es][:, :, c0:], AF.Ln, bias=1.0), reads=[("e", es)], writes=[("sp", es)])
                if j is not None:
                    for h2 in range(2):
                        R.op("dve", f_tt(sp3[es][:, h2, c0:], sp3[es][:, h2, c0:], mask[:, j, c0:], ALU.mult),
                             reads=[("sp", es), "cbf"], writes=[("sp", es)])
                so, sn = n % 3, (n + 1) % 3
                if not last:
                    if first:
                        R.op("dve", f_copy(spsum[sn][:, :, c0:], sp3[es][:, :, c0:]), reads=[("sp", es)], writes=[("spsum", sn)])
                    else:
                        R.op("dve", f_tt(spsum[sn][:, :, c0:], spsum[so][:, :, c0:], sp3[es][:, :, c0:], ALU.add),
                             reads=[("sp", es), ("spsum", so)], writes=[("spsum", sn)])
                    if j is not None and j >= 1:
                        R.op("dve", f_memset(spsum[sn][:, :, c0 - 128:c0], 0.0), writes=[("spsum", sn)])

            def S2(n):
                pr, g, kb, idx, first, last = its[n]
                j, c0 = geom(n)
                es = n % 3
                z0 = zb(n, 0)
                so = n % 3
                for h2 in range(2):
                    R.mm(psum[:, z0 + h2, c0:], [(negtri, sp3[es][:, h2, c0:])], reads=[("sp", es), "cbf"],
                         writes=[("ps", z0 + h2)], start=False, stop=first, skip_group_check=True)
                    if not first:
                        R.mm(psum[:, z0 + h2, c0:], [(negone, spsum[so][:, h2, c0:])], reads=[("spsum", so), "cbf"],
                             writes=[("ps", z0 + h2)], start=False, stop=True, skip_group_check=True)
                R.op("act", f_act(a3[es][:, :, c0:], psum[:, z0:z0 + 2, c0:], AF.Exp),
                     reads=[("ps", z0), ("ps", z0 + 1)], writes=[("a", es)])
                if j is not None:
                    for h2 in range(2):
                        R.op("dve", f_tt(a3[es][:, h2, c0:], a3[es][:, h2, c0:], mask[:, j, c0:], ALU.mult),
                             reads=[("a", es), "cbf"], writes=[("a", es)])

            def S3(n):
                pr, g, kb, idx, first, last = its[n]
                j, c0 = geom(n)
                es = n % 3
                yb = 6 + (pr * 8 + g) % 2
                for h2 in range(2):
                    hh = 2 * pr + h2
                    R.mm(psum[h2 * 64:(h2 + 1) * 64, yb, c0:], [(vb[:, kb, hh * 64:(hh + 1) * 64], a3[es][:, h2, c0:])],
                         reads=[("v", kb), ("a", es)], writes=[("ps", yb, h2)], start=first, stop=last, skip_group_check=True)
                if last:
                    R.op("dve", f_copy(qb[:, pr, g * TT:(g + 1) * TT], ps(yb)),
                         reads=[("ps", yb, 0), ("ps", yb, 1)], writes=[("q", pr, g)])

            for step in range(N + 2):
                if step < N:
                    S1(step)
                if 0 <= step - 1 < N:
                    S2(step - 1)
                if 0 <= step - 2 < N:
                    S3(step - 2)

        def mixout_phase(l):
            win_d = w_in[l].rearrange("(k p) n -> p k n", p=128)
            wo_d = w_out[l].rearrange("(k p) n -> p k n", p=128)
            for s in range(2):
                R.dma("pool", "wg%d" % s, out=Wcp[:, :, s * 512:(s + 1) * 512], in_=win_d[:, :, 1536 + s * 512:1536 + (s + 1) * 512],
                      writes=[("Wcp", s)])
            for s in range(2):
                R.dma("pool", "wu%d" % s, out=Wout[:, :, s * 512:(s + 1) * 512], in_=wo_d[:, :, s * 512:(s + 1) * 512],
                      writes=[("Wout", s)])
            R.dma("pool", "wd0", out=Pw, in_=pwblk[l].rearrange("p (a b) -> p a b", a=2), writes=["Pw"])
            R.op("dve", f_memset(ccu[:, :, 0:2], 0.0), writes=[("ccu", 0), ("ccu", 1)])
            R.op("dve", f_memset(pbuf[:, :, 0:16], 0.0), writes=[("pbuf", 0), ("pbuf", 1)])
            load_x(0, 0)
            load_x(1, 1)
            norm(16, 24, 0, 0)
            for k in range(KC):
                norm_sq(k, 1)
            for i in range(NT):
                b = i % 2
                bx = i % 3
                hb = hs[b]
                HB = H_ALLb(b)
                if i + 2 < NT:
                    load_x(i + 2, (i + 2) % 3)
                nxt = i + 1 < NT
                for pos, c in enumerate((6, 7, 2, 0, 4, 3, 1, 5)):
                    pb = pos % 4
                    R.mm(ps(pb), [(Wcp[:, k, c * 128:(c + 1) * 128], hb[:, k, :]) for k in range(KC)],
                         reads=[("Wcp", c // 4)] + HB, writes=[("ps", pb)])
                    if nxt and pos < 4:
                        norm_acc(2 * pos)
                        norm_acc(2 * pos + 1)
                    if nxt and pos == 4:
                        norm_rstd()
                        if i + 2 < NT:
                            for k in range(KC):
                                norm_sq(k, (i + 2) % 3)
                    if c < 2:
                        R.op("act", f_act(cbs[:, c, :], ps(pb), AF.Copy), reads=[("ps", pb)], writes=[("cbs", c)])
                    elif c < 4:
                        R.op("act", f_act(ccs[:, c - 2, :], ps(pb), AF.Copy), reads=[("ps", pb)], writes=[("ccs", c - 2)])
                    elif c < 6:
                        cc = c - 4
                        R.op("dve", f_tt(ccu[:, cc, 2:514], ps(pb), ccs[:, cc, :], ALU.mult),
                             reads=[("ps", pb), ("ccs", cc)], writes=[("ccu", cc)])
                        R.op("dve", f_ts(acc, ccu[:, cc, 0:512], cws[:, cc * 3:cc * 3 + 1], None, ALU.mult, ALU.bypass),
                             reads=[("ccu", cc), "cws"], writes=["acc"])
                        R.op("dve", f_stt(acc, ccu[:, cc, 1:513], cws[:, cc * 3 + 1:cc * 3 + 2], acc, ALU.mult, ALU.add),
                             reads=[("ccu", cc), "cws", "acc"], writes=["acc"])
                        R.op("dve", f_stt(acc, ccu[:, cc, 2:514], cws[:, cc * 3 + 2:cc * 3 + 3], acc, ALU.mult, ALU.add),
                             reads=[("ccu", cc), "cws", "acc"], writes=["acc"])
                        R.op("dve", f_tt(ymix[:, cc, :], acc, cbs[:, cc, :], ALU.mult),
                             reads=["acc", ("cbs", cc)], writes=[("ymix", cc)])
                        R.op("dve", f_copy(ccu[:, cc, 0:2], ccu[:, cc, 512:514]), reads=[("ccu", cc)], writes=[("ccu", cc)])
                    else:
                        pc = c - 6
                        pooled = pooled2[:, pc, :]
                        R.op("act", f_act(pbuf[:, pc, 16:528], ps(pb), AF.Copy), reads=[("ps", pb)], writes=[("pbuf", pc)])
                        P = pbuf[:, pc, :]
                        R.op("dve", f_tt(s2[:, 1:528], P[:, 1:528], P[:, 0:527], ALU.add), reads=[("pbuf", pc)], writes=["s2"])
                        lo, hi = slice(0, 64), slice(64, 128)
                        if pc == 0:
                            R.op("dve", f_tt(s4[hi, 3:528], s2[hi, 3:528], s2[hi, 1:526], ALU.add), reads=["s2"], writes=["s4"])
                            Slo, Shi = s2, s4
                        else:
                            R.op("dve", f_tt(s4[:, 3:528], s2[:, 3:528], s2[:, 1:526], ALU.add), reads=["s2"], writes=["s4"])
                            R.op("dve", f_tt(s8[:, 7:528], s4[:, 7:528], s4[:, 3:524], ALU.add), reads=["s4"], writes=["s8"])
                            R.op("dve", f_tt(s16[hi, 15:528], s8[hi, 15:528], s8[hi, 7:520], ALU.add), reads=["s8"], writes=["s16"])
                            Slo, Shi = s8, s16
                        rd = [("pbuf", pc), "s2", "s4", "s8", "s16", "cf32"]
                        R.op("dve", f_stt(pooled[lo, :], Slo[lo, 16:528], invw[lo, pc:pc + 1], P[lo, 16:528], ALU.mult, ALU.subtract),
                             reads=rd, writes=[("pooled", pc)])
                        R.op("dve", f_stt(pooled[hi, :], Shi[hi, 16:528], invw[hi, pc:pc + 1], P[hi, 16:528], ALU.mult, ALU.subtract),
                             reads=rd, writes=[("pooled", pc)])
                        if i == 0:
                            for half, S_ in ((lo, Slo), (hi, Shi)):
                                R.op("dve", f_tt(acc[half, 0:16], S_[half, 16:32], invcnt[half, pc, :], ALU.mult),
                                     reads=rd, writes=["acc"])
                                R.op("dve", f_tt(pooled[half, 0:16], acc[half, 0:16], P[half, 16:32], ALU.subtract),
                                     reads=rd + ["acc"], writes=[("pooled", pc)])
                        R.op("dve", f_copy(pbuf[:, pc, 0:16], pbuf[:, pc, 512:528]), reads=[("pbuf", pc)], writes=[("pbuf", pc)])
                for pc in range(2):
                    R.mm(ps(pc), [(Pw[:, pc, :], pooled2[:, pc, :])], reads=["Pw", ("pooled", pc)], writes=[("ps", pc)])
                    R.op("dve", f_ts(ymix[:, 2 + pc, :], ps(pc), pscs[:, pc:pc + 1], None, ALU.mult, ALU.bypass),
                         reads=[("ps", pc), "pscs"], writes=[("ymix", 2 + pc)])
                for dk in range(KC):
                    ob = 4 + dk % 2
                    pairs = []
                    for mk in range(8):
                        rhs = qb[:, mk, i * TT:(i + 1) * TT] if mk < 4 else ymix[:, mk - 4, :]
                        pairs.append((Wout[:, mk, dk * 128:(dk + 1) * 128], rhs))
                    R.mm(ps(ob), pairs, reads=[("Wout", dk // 4)] + [("ymix", m) for m in range(4)] + [("q", m, i) for m in range(4)],
                         writes=[("ps", ob)])
                    xbi = dk % 2
                    R.op("dve", f_stt(xsb[:, xbi, :], ps(ob), AG[:, 24 + dk:24 + dk + 1], xts[bx][:, dk, :], ALU.mult, ALU.add),
                         reads=[("ps", ob), ("xt", bx), "AG"], writes=[("xsb", xbi)])
                    R.dma("sp", "xst%d" % xbi, out=xs_d[dk, :, i * TT:(i + 1) * TT], in_=xsb[:, xbi, :], reads=[("xsb", xbi)],
                          writes=[xname(i, dk)])
                    if nxt:
                        norm_out(dk, 16, 24, (i + 1) % 3, 1 - b)

        phases = []
        for l in range(n_layers):
            phases += [("ada", l), ("ffn1", l), ("proj", l), ("attn", l), ("mixout", l), ("ffn2", l)]
        if stop_after is not None:
            phases = phases[:phases.index(tuple(stop_after)) + 1]
        final = phases[-1]
        for ph in phases:
            name, l = ph
            R.barrier()
            if name == "ada":
                ada_phase(l)
            elif name == "ffn1":
                ffn_phase(l, 1, xs_d, True)
                state["src"], state["src_is_scr"] = xs_d, True
            elif name == "proj":
                proj_phase(l)
            elif name == "attn":
                attn_phase()
            elif name == "mixout":
                mixout_phase(l)
            elif name == "ffn2":
                if ph == final and stop_after is None:
                    ffn_phase(l, 2, yT, False)
                else:
                    ffn_phase(l, 2, xs_d, True)
        R.barrier()
        if stop_after is not None:
            for dk in range(KC):
                R.dma("sp", "xst0", out=yT[dk], in_=xs_d[dk])
            qd = nc.dram_tensor("qdump", [128, 24576], F32, kind="ExternalOutput").ap()
            R.dma("sp", "xst0", out=qd, in_=arena[:, 0:24576])
            md = nc.dram_tensor("mdump", [128, 120], F32, kind="ExternalOutput").ap()
            R.dma("sp", "xst0", out=md[:, 0:72], in_=modv)
            R.dma("sp", "xst0", out=md[:, 72:120], in_=AG)
            R.barrier()

        block = es.enter_context(nc.Block())

        @block.tensor
        def _(e):
            for f in R.q["pe"]:
                f(e)

        @block.scalar
        def _(e):
            for f in R.q["act"]:
                f(e)

        @block.vector
        def _(e):
            for f in R.q["dve"]:
                f(e)

        @block.gpsimd
        def _(e):
            for f in R.q["pool"]:
                f(e)

        @block.sync
        def _(e):
            for f in R.q["sp"]:
                f(e)

    return nc, R.ninstr


def make_consts():
    c = np.zeros((128, NCST), np.float32)
    j = np.arange(128)[:, None]
    s = np.arange(128)[None, :]
    c[:, C_NEGTRI:C_NEGTRI + 128] = -(j >= s).astype(np.float32)
    c[:, C_NEGONE:C_NEGONE + 128] = -1.0
    c[:, C_BLK:C_BLK + 128] = ((j // 64) == (s // 64)).astype(np.float32)
    c[:, C_ONES:C_ONES + 128] = 1.0
    t = np.arange(512)[None, :]
    for jj in range(4):
        c[:, C_MASK + jj * 512:C_MASK + (jj + 1) * 512] = (t > 128 * jj + j).astype(np.float32)
    wins = (2, 4, 8, 16)
    for pc in range(2):
        for half in range(2):
            w = wins[pc * 2 + half]
            rows = slice(half * 64, half * 64 + 64)
            c[rows, C_INVCNT + pc * 16:C_INVCNT + (pc + 1) * 16] = 1.0 / np.minimum(np.arange(1, 17), w)
            c[rows, C_INVW + pc] = 1.0 / w
    return c


def prep_shared(inp):
    f = lambda a: np.ascontiguousarray(np.asarray(a, dtype=np.float32))
    sh = {}
    sh["cst"] = make_consts()
    sh["w_ada"] = f(inp["w_ada"])
    sh["b_adaT"] = f(np.asarray(inp["b_ada"]).reshape(L, 72, 128).transpose(0, 2, 1))
    nr = np.stack([np.asarray(inp["ffn1_norm"]), np.asarray(inp["mix_norm"]), np.asarray(inp["ffn2_norm"])], axis=1)
    sh["norms"] = f(nr.reshape(L, 3, 8, 128).transpose(0, 3, 1, 2).reshape(L, 128, 24))
    for k in ("ffn1_gate", "ffn1_up", "ffn1_down", "ffn2_gate", "ffn2_up", "ffn2_down", "w_in", "w_out"):
        sh[k] = f(inp[k])
    qn = np.asarray(inp["q_norm"])
    kn = np.asarray(inp["k_norm"])
    sh["qkg"] = f(np.stack([np.tile(qn, (1, 2)), np.tile(kn, (1, 2))], axis=2))
    cw = np.asarray(inp["conv_w"])
    sh["convw"] = f(cw.reshape(L, 3, 2, 128).transpose(0, 3, 2, 1).reshape(L, 128, 6))
    pw = np.asarray(inp["pool_w"])
    blkm = np.zeros((L, 128, 2, 128), np.float32)
    for pc in range(2):
        for half in range(2):
            blkm[:, half * 64:(half + 1) * 64, pc, half * 64:(half + 1) * 64] = pw[:, pc * 2 + half]
    sh["pwblk"] = f(blkm.reshape(L, 128, 256))
    sh["pscale"] = f(np.asarray(inp["pool_scale"]).reshape(L, 2, 128).transpose(0, 2, 1))
    return sh


def prep_core(inp, b):
    x = np.asarray(inp["x"][b], dtype=np.float32)
    xT = np.ascontiguousarray(x.T).reshape(KC, 128, T)
    c = np.asarray(inp["c"][b], dtype=np.float32)
    return {"xT": xT, "ccol": np.ascontiguousarray(c.reshape(KC, 128).T)}


_CACHE = {}


def kernel(**inputs):
    if "nc" not in _CACHE:
        _CACHE["nc"] = build()[0]
    nc = _CACHE["nc"]
    sh = prep_shared(inputs)
    B = np.asarray(inputs["x"]).shape[0]
    in_maps = []
    for b in range(B):
        m = dict(sh)
        m.update(prep_core(inputs, b))
        in_maps.append(m)
    res = run_bass_kernel_spmd(nc, in_maps, core_ids=list(range(B)))
    out = np.empty((B, T, D), np.float32)
    for b in range(B):
        out[b] = res.results[b]["yT"].reshape(D, T).T
    return out
```
